# Optimizing a Trainium2 kernel written in Bass

```python
import math
import jax
import jax.numpy as jnp
from jax import lax
import numpy as np


D_MODEL = 2048
BATCH = 2
SEQ = 16384
DEPTH = 2

N_A_LAYERS = DEPTH // 2
N_B_LAYERS = DEPTH - N_A_LAYERS
SSM_GROUP = 16
SSM_GROUPS = D_MODEL // SSM_GROUP
SSM_STATE = 64
SCAN_CHUNK = 128
HEAD_DIM = 128
N_HEADS = D_MODEL // HEAD_DIM
DILATION_CFG = ((128, 1), (512, 4), (2048, 16))
N_GROUPS = len(DILATION_CFG)
ATTN_BLOCK = 128
D_FF = -(-8 * D_MODEL // (3 * 256)) * 256
EPS = 1e-6

kernel_name = 'yoco_s5_dilated_window_hybrid'


def rmsnorm(x, g):
    xf = x.astype(jnp.float32)
    y = xf * lax.rsqrt(jnp.mean(xf * xf, axis=-1, keepdims=True) + EPS) * g.astype(jnp.float32)
    return y.astype(x.dtype)


def swiglu_ffn(h, w_gate_up, w_down):
    gate, up = jnp.split(h @ w_gate_up, 2, axis=-1)
    return (jax.nn.silu(gate) * up) @ w_down


def _complex_scan_combine(e1, e2):
    a1r, a1i, b1r, b1i = e1
    a2r, a2i, b2r, b2i = e2
    ar = a2r * a1r - a2i * a1i
    ai = a2r * a1i + a2i * a1r
    br = a2r * b1r - a2i * b1i + b2r
    bi = a2r * b1i + a2i * b1r + b2i
    return (ar, ai, br, bi)


def s5_mixer(u, a_re, a_im, log_dt, b_re, b_im, c_re, c_im, d_skip, w_glu):
    bsz, seqlen, _ = u.shape
    f32 = jnp.float32
    lam_re = jnp.minimum(a_re.astype(f32), -1e-4)
    lam_im = a_im.astype(f32)
    dt = jnp.exp(log_dt.astype(f32))[:, None]
    mag = jnp.exp(lam_re * dt)
    ang = lam_im * dt
    lb_re = mag * jnp.cos(ang)
    lb_im = mag * jnp.sin(ang)
    den = lam_re * lam_re + lam_im * lam_im
    nr = lb_re - 1.0
    ni = lb_im
    coef_re = (nr * lam_re + ni * lam_im) / den
    coef_im = (ni * lam_re - nr * lam_im) / den
    br = b_re.astype(f32)
    bi = b_im.astype(f32)
    bb_re = coef_re[..., None] * br - coef_im[..., None] * bi
    bb_im = coef_re[..., None] * bi + coef_im[..., None] * br
    cr = c_re.astype(f32)
    ci = c_im.astype(f32)
    a_shape = (bsz, SCAN_CHUNK, SSM_GROUPS, SSM_STATE)
    a_el_re = jnp.broadcast_to(lb_re, a_shape)
    a_el_im = jnp.broadcast_to(lb_im, a_shape)

    uf = u.astype(f32)
    n_chunks = seqlen // SCAN_CHUNK
    u_chunks = uf.reshape(bsz, n_chunks, SCAN_CHUNK, SSM_GROUPS, SSM_GROUP).transpose(1, 0, 2, 3, 4)

    def step(carry, u_c):
        h_re, h_im = carry
        bu_re = jnp.einsum('blgh,gph->blgp', u_c, bb_re)
        bu_im = jnp.einsum('blgh,gph->blgp', u_c, bb_im)
        ar, ai, sr, si = lax.associative_scan(_complex_scan_combine, (a_el_re, a_el_im, bu_re, bu_im), axis=1)
        st_re = ar * h_re[:, None] - ai * h_im[:, None] + sr
        st_im = ar * h_im[:, None] + ai * h_re[:, None] + si
        y = jnp.einsum('blgp,ghp->blgh', st_re, cr) - jnp.einsum('blgp,ghp->blgh', st_im, ci)
        return (st_re[:, -1], st_im[:, -1]), y

    h0 = (jnp.zeros((bsz, SSM_GROUPS, SSM_STATE), f32), jnp.zeros((bsz, SSM_GROUPS, SSM_STATE), f32))
    _, y = lax.scan(step, h0, u_chunks)
    y = y.transpose(1, 0, 2, 3, 4).reshape(bsz, seqlen, D_MODEL) + d_skip.astype(f32) * uf
    g = jax.nn.gelu(y).astype(u.dtype)
    val, gate = jnp.split(g @ w_glu, 2, axis=-1)
    return (val.astype(f32) * jax.nn.sigmoid(gate.astype(f32))).astype(u.dtype)


def alibi_slopes():
    n = N_GROUPS * N_HEADS
    i = jnp.arange(1, n + 1, dtype=jnp.float32)
    s = jnp.exp2(-8.0 * i / n)
    return s.reshape(N_HEADS, N_GROUPS).T


def head_rmsnorm(t, g):
    tf = t.astype(jnp.float32)
    y = tf * lax.rsqrt(jnp.mean(tf * tf, axis=-1, keepdims=True) + EPS) * g.astype(jnp.float32)
    return y.astype(t.dtype)


def shared_kv(x, kv_norm, w_kv, k_norm):
    bsz, seqlen, _ = x.shape
    kv = (rmsnorm(x, kv_norm) @ w_kv).reshape(bsz, seqlen, 2, N_HEADS, HEAD_DIM)
    k = head_rmsnorm(kv[:, :, 0], k_norm)
    v = kv[:, :, 1]
    return k, v


def dilated_window_group(q, k, v, window, dilation, slopes):
    bsz, seqlen, nh, dh = q.shape
    span = dilation * ATTN_BLOCK
    lp = -(-seqlen // span) * span
    pad = lp - seqlen
    n_sub = lp // dilation
    nb = n_sub // ATTN_BLOCK
    w_sub = window // dilation

    def to_blocks(t):
        t = jnp.pad(t, ((0, 0), (0, pad), (0, 0), (0, 0)))
        t = t.reshape(bsz, n_sub, dilation, nh, dh).transpose(0, 2, 1, 3, 4)
        return t.reshape(bsz, dilation, nb, ATTN_BLOCK, nh, dh)

    def with_prev(t):
        prev = jnp.pad(t[:, :, :-1], ((0, 0), (0, 0), (1, 0), (0, 0), (0, 0), (0, 0)))
        return jnp.concatenate([prev, t], axis=3)

    qb = to_blocks(q)
    kk = with_prev(to_blocks(k))
    vv = with_prev(to_blocks(v))
    s = jnp.einsum('bdnqhe,bdnkhe->bdnhqk', qb, kk, preferred_element_type=jnp.float32) * (dh ** -0.5)
    qi = jnp.arange(ATTN_BLOCK)[:, None] + ATTN_BLOCK
    kj = jnp.arange(2 * ATTN_BLOCK)[None, :]
    dist = qi - kj
    valid = (dist >= 0) & (dist <= w_sub)
    valid = valid[None] & ((jnp.arange(nb)[:, None, None] > 0) | (kj >= ATTN_BLOCK)[None])
    bias = -slopes[:, None, None] * (dilation * dist).astype(jnp.float32)[None]
    s = jnp.where(valid[:, None], s + bias, -jnp.inf)
    lse = jax.nn.logsumexp(s, axis=-1)
    p = jnp.exp(s - lse[..., None])
    o = jnp.einsum('bdnhqk,bdnkhe->bdnqhe', p.astype(v.dtype), vv, preferred_element_type=jnp.float32)
    o = o.reshape(bsz, dilation, n_sub, nh, dh).transpose(0, 2, 1, 3, 4).reshape(bsz, lp, nh, dh)[:, :seqlen]
    lse = lse.transpose(0, 1, 2, 4, 3).reshape(bsz, dilation, n_sub, nh)
    lse = lse.transpose(0, 2, 1, 3).reshape(bsz, lp, nh)[:, :seqlen]
    return o, lse


def dilated_attention(h, k, v, q_norm_l, w_q_l, w_o_l):
    bsz, seqlen, _ = h.shape
    q = (h @ w_q_l).reshape(bsz, seqlen, N_GROUPS, N_HEADS, HEAD_DIM)
    q = head_rmsnorm(q, q_norm_l[:, None, :])
    slopes = alibi_slopes()
    outs = []
    lses = []
    for g, (win, dil) in enumerate(DILATION_CFG):
        o, lse = dilated_window_group(q[:, :, g], k, v, win, dil, slopes[g])
        outs.append(o)
        lses.append(lse)
    wts = jax.nn.softmax(jnp.stack(lses, axis=0), axis=0)
    merged = jnp.sum(wts[..., None] * jnp.stack(outs, axis=0), axis=0)
    return merged.reshape(bsz, seqlen, N_HEADS * HEAD_DIM).astype(h.dtype) @ w_o_l


def setup_inputs(seed: int = 0) -> dict:
    key = jax.random.key(seed)
    ks = jax.random.split(key, 24)
    f32 = jnp.float32
    d = D_MODEL
    qkv_w = N_HEADS * HEAD_DIM
    nrm = lambda k, shape, scale: jax.random.normal(k, shape, f32) * scale
    n_idx = jnp.arange(SSM_STATE, dtype=f32)
    x = jax.random.normal(ks[0], (BATCH, SEQ, d), f32)
    mix_norm = 1.0 + nrm(ks[1], (DEPTH, d), 0.02)
    ffn_norm = 1.0 + nrm(ks[2], (DEPTH, d), 0.02)
    ssm_a_re = -0.5 + nrm(ks[3], (N_A_LAYERS, SSM_GROUPS, SSM_STATE), 0.01)
    ssm_a_im = math.pi * n_idx + nrm(ks[4], (N_A_LAYERS, SSM_GROUPS, SSM_STATE), 0.01)
    ssm_log_dt = jax.random.uniform(ks[5], (N_A_LAYERS, SSM_GROUPS), f32, math.log(1e-3), math.log(1e-1))
    ssm_b_re = nrm(ks[6], (N_A_LAYERS, SSM_GROUPS, SSM_STATE, SSM_GROUP), SSM_GROUP ** -0.5)
    ssm_b_im = nrm(ks[7], (N_A_LAYERS, SSM_GROUPS, SSM_STATE, SSM_GROUP), SSM_GROUP ** -0.5)
    ssm_c_re = nrm(ks[8], (N_A_LAYERS, SSM_GROUPS, SSM_GROUP, SSM_STATE), SSM_STATE ** -0.5)
    ssm_c_im = nrm(ks[9], (N_A_LAYERS, SSM_GROUPS, SSM_GROUP, SSM_STATE), SSM_STATE ** -0.5)
    ssm_d = nrm(ks[10], (N_A_LAYERS, d), 1.0)
    w_glu = nrm(ks[11], (N_A_LAYERS, d, 2 * d), d ** -0.5)
    kv_norm = 1.0 + nrm(ks[12], (d,), 0.02)
    w_kv = nrm(ks[13], (d, 2 * qkv_w), d ** -0.5)
    k_norm = 1.0 + nrm(ks[14], (HEAD_DIM,), 0.02)
    w_q = nrm(ks[15], (N_B_LAYERS, d, N_GROUPS * qkv_w), d ** -0.5)
    q_norm = 1.0 + nrm(ks[16], (N_B_LAYERS, N_GROUPS, HEAD_DIM), 0.02)
    w_o = nrm(ks[17], (N_B_LAYERS, qkv_w, d), qkv_w ** -0.5)
    w_gate_up = nrm(ks[18], (DEPTH, d, 2 * D_FF), d ** -0.5)
    w_down = nrm(ks[19], (DEPTH, D_FF, d), D_FF ** -0.5)
    return {'x': x, 'mix_norm': mix_norm, 'ffn_norm': ffn_norm,
            'ssm_a_re': ssm_a_re, 'ssm_a_im': ssm_a_im, 'ssm_log_dt': ssm_log_dt,
            'ssm_b_re': ssm_b_re, 'ssm_b_im': ssm_b_im, 'ssm_c_re': ssm_c_re, 'ssm_c_im': ssm_c_im,
            'ssm_d': ssm_d, 'w_glu': w_glu, 'kv_norm': kv_norm, 'w_kv': w_kv, 'k_norm': k_norm,
            'w_q': w_q, 'q_norm': q_norm, 'w_o': w_o, 'w_gate_up': w_gate_up, 'w_down': w_down}


def reference(x, mix_norm, ffn_norm, ssm_a_re, ssm_a_im, ssm_log_dt, ssm_b_re, ssm_b_im, ssm_c_re, ssm_c_im,
              ssm_d, w_glu, kv_norm, w_kv, k_norm, w_q, q_norm, w_o, w_gate_up, w_down):
    k = None
    v = None
    for layer in range(DEPTH):
        h = rmsnorm(x, mix_norm[layer])
        if layer < N_A_LAYERS:
            i = layer
            x = x + s5_mixer(h, ssm_a_re[i], ssm_a_im[i], ssm_log_dt[i], ssm_b_re[i], ssm_b_im[i],
                             ssm_c_re[i], ssm_c_im[i], ssm_d[i], w_glu[i])
        else:
            if layer == N_A_LAYERS:
                k, v = shared_kv(x, kv_norm, w_kv, k_norm)
            j = layer - N_A_LAYERS
            x = x + dilated_attention(h, k, v, q_norm[j], w_q[j], w_o[j])
        x = x + swiglu_ffn(rmsnorm(x, ffn_norm[layer]), w_gate_up[layer], w_down[layer])
    return x
```

```python
import contextlib
import numpy as np
import ml_dtypes
import concourse.bass as bass
import concourse.mybir as mybir
from concourse.bass_utils import run_bass_kernel_spmd

F32 = mybir.dt.float32
BF16 = mybir.dt.bfloat16
I32 = mybir.dt.int32
AF = mybir.ActivationFunctionType
ALU = mybir.AluOpType
NPBF = ml_dtypes.bfloat16

NCORES = 8
D = 2048
KT = 16
DFF = 5632
FT = 44
SEQ = 16384
BATCH = 2
TPC = 4096
NT = 512
EPS = 1e-6
EPOCH = 12000
WSLOT = 11264

ENGS = ["pe", "act", "dve", "pool", "sp"]


class DSem:
    def __init__(self, sem):
        self.sem = sem
        self.count = 0


class Prog:
    def __init__(self, nc):
        self.nc = nc
        self.stack = contextlib.ExitStack()
        self.q = {e: [] for e in ENGS}
        self.cnt = {e: 0 for e in ENGS}
        self.epoch = {e: 0 for e in ENGS}
        self.esems = {e: [self.stack.enter_context(nc.semaphore("s_%s_0" % e))] for e in ENGS}
        self.nsb = 0
        self.finals = []

    def sb(self, name, shape, dt):
        return self.stack.enter_context(self.nc.sbuf_tensor(name, shape, dt))

    def ps(self, name, shape, dt=F32):
        return self.stack.enter_context(self.nc.psum_tensor(name, shape, dt))

    def dsem(self, name):
        return DSem(self.stack.enter_context(self.nc.semaphore(name)))

    def op(self, eng, fn, waits=()):
        if self.cnt[eng] >= EPOCH:
            self.epoch[eng] += 1
            self.cnt[eng] = 0
            self.esems[eng].append(
                self.stack.enter_context(self.nc.semaphore("s_%s_%d" % (eng, self.epoch[eng]))))
        self.cnt[eng] += 1
        tok = ("e", eng, self.epoch[eng], self.cnt[eng])
        self.q[eng].append((fn, [w for w in waits if w is not None], tok))
        return tok

    def dma(self, eng, dsem, out, in_, waits=()):
        dsem.count += 16
        tok = ("d", dsem, 0, dsem.count)
        self.q[eng].append((lambda e: e.dma_start(out=out, in_=in_),
                            [w for w in waits if w is not None], tok))
        return tok

    def fence(self, eng, waits):
        self.q[eng].append((None, [w for w in waits if w is not None], None))

    def replay(self, eng, e):
        known = {}
        for fn, waits, tok in self.q[eng]:
            need = {}
            for w in waits:
                key = (w[0], w[1] if w[0] == "e" else id(w[1]))
                cur = need.get(key)
                if cur is None or (w[2], w[3]) > (cur[2], cur[3]):
                    need[key] = w
            for key, w in need.items():
                k = known.get(key)
                if k is not None and k >= (w[2], w[3]):
                    continue
                known[key] = (w[2], w[3])
                sem = self.esems[w[1]][w[2]] if w[0] == "e" else w[1].sem
                e.wait_ge(sem, w[3])
            if fn is None:
                continue
            ins = fn(e)
            if tok[0] == "e":
                ins.then_inc(self.esems[eng][tok[2]], 1)
            else:
                ins.then_inc(tok[1].sem, 16)

    def finish(self):
        nc = self.nc
        with nc.Block() as block:
            @block.tensor
            def _(e):
                self.replay("pe", e)

            @block.scalar
            def _(e):
                self.replay("act", e)

            @block.vector
            def _(e):
                self.replay("dve", e)

            @block.gpsimd
            def _(e):
                self.replay("pool", e)

            @block.sync
            def _(e):
                self.replay("sp", e)
        self.stack.close()


class WRing:
    def __init__(self, P, nslots=2, nelem=WSLOT, queue="sp"):
        self.P = P
        self.n = nslots
        self.queue = queue
        self.tiles = [P.sb("wslot%d" % i, [128, nelem], BF16) for i in range(nslots)]
        self.sems = [P.dsem("wsem%d" % i) for i in range(nslots)]
        self.free = [None] * nslots
        self.i = 0

    def load(self, src_ap, nelem):
        s = self.i % self.n
        self.i += 1
        t = self.tiles[s]
        tok = self.P.dma(self.queue, self.sems[s], t[:, 0:nelem], src_ap, waits=[self.free[s]])
        return s, t, tok

    def release(self, s, tok):
        self.free[s] = tok


def mm(out, lhsT, rhs, start, stop):
    return lambda e: e.matmul(out, lhsT, rhs, start=start, stop=stop)


class Ctx:
    pass


def setup_common(P, C):
    nc = P.nc
    C.ones_bf = P.sb("ones_bf", [128, 128], BF16)
    C.t_init = P.op("dve", lambda e: e.memset(C.ones_bf[:], 1.0))
    C.sq = [P.sb("sq%d" % i, [128, NT], BF16) for i in range(2)]
    C.sq_free = [None, None]
    C.sq_i = 0
    C.rstd = P.sb("rstd", [128, NT], F32)
    C.rtmp = P.sb("rtmp", [128, NT], F32)
    C.ps_stat = P.ps("ps_stat", [128, NT])
    C.stat_free = None
    C.rstd_readers = []


def emit_sumsq_rstd(P, C, src_fn, nk, src_tok, inv_n):
    last = None
    for k in range(nk):
        i = C.sq_i % 2
        C.sq_i += 1
        sq = C.sq[i]
        src = src_fn(k)
        ta = P.op("act", lambda e, sq=sq, src=src: e.activation(out=sq[:], in_=src, func=AF.Square),
                  waits=[src_tok, C.sq_free[i]])
        last = P.op("pe", mm(C.ps_stat[:], C.ones_bf[:], sq[:], k == 0, k == nk - 1),
                    waits=[ta, C.t_init] + ([C.stat_free] if k == 0 else []))
        C.sq_free[i] = last
    t1 = P.op("act", lambda e: e.activation(out=C.rtmp[:], in_=C.ps_stat[:], func=AF.Sqrt,
                                            bias=EPS, scale=inv_n),
              waits=[last] + C.rstd_readers)
    C.stat_free = t1
    t2 = P.op("dve", lambda e: e.reciprocal(out=C.rstd[:], in_=C.rtmp[:]), waits=[t1] + C.rstd_readers)
    C.rstd_readers = []
    return t2


def emit_apply_norm(P, C, xs, gain, gcol0, out_bf, rstd_tok, xs_tok, out_free):
    last = None
    for kt in range(KT):
        last = P.op("dve", lambda e, kt=kt: e.scalar_tensor_tensor(
            out=out_bf[:, kt, :], in0=xs[:, kt, :], scalar=gain[:, gcol0 + kt:gcol0 + kt + 1],
            in1=C.rstd[:], op0=ALU.mult, op1=ALU.mult),
            waits=[rstd_tok, xs_tok, out_free, C.t_gn] if kt == 0 else [])
    C.rstd_readers.append(last)
    return last


def emit_dual_linear(P, C, W, act_in, act_tok, wsrc, nblocks, epilogue):
    last_mm = None
    for mb in range(nblocks):
        s, wt, wtok = W.load(wsrc(mb), KT * 512)
        for h in range(2):
            i = C.dl_i % 2
            C.dl_i += 1
            pa, pb = C.psA[i], C.psB[i]
            for kt in range(KT):
                ta = P.op("pe", mm(pa[:], wt[:, kt * 512 + h * 128: kt * 512 + h * 128 + 128],
                                   act_in[:, kt, :], kt == 0, kt == KT - 1),
                          waits=[wtok, act_tok, C.relA[i]] if kt == 0 else [])
            for kt in range(KT):
                tb = P.op("pe", mm(pb[:], wt[:, kt * 512 + 256 + h * 128: kt * 512 + 256 + h * 128 + 128],
                                   act_in[:, kt, :], kt == 0, kt == KT - 1),
                          waits=[C.relB[i]] if kt == 0 else [])
            C.relA[i], C.relB[i] = epilogue(mb * 2 + h, pa, pb, ta, tb)
            last_mm = tb
        W.release(s, last_mm)
    return last_mm


def emit_ffn(P, C, W, xs, xs_tok, gain, gcol0, wgu, wdn):
    rstd_tok = emit_sumsq_rstd(P, C, lambda k: xs[:, k, :], KT, xs_tok, 1.0 / D)
    xh_tok = emit_apply_norm(P, C, xs, gain, gcol0, C.xh, rstd_tok, xs_tok, C.xh_free)
    hid_last = [None]

    def epi(m, pa, pb, ta, tb):
        i = C.tmp_i % 2
        C.tmp_i += 1
        tmp = C.tmpS[i]
        t1 = P.op("act", lambda e: e.activation(out=tmp[:], in_=pa[:], func=AF.Silu),
                  waits=[ta, C.tmpS_free[i]])
        t2 = P.op("dve", lambda e: e.tensor_tensor(out=C.hid[:, m, :], in0=tmp[:], in1=pb[:], op=ALU.mult),
                  waits=([t1, tb] + C.hid_free) if m == 0 else [t1, tb])
        C.tmpS_free[i] = t2
        hid_last[0] = t2
        return t1, t2

    lm = emit_dual_linear(P, C, W, C.xh, xh_tok, wgu, FT // 2, epi)
    C.xh_free = lm
    last_w = None
    last_mm = None
    for ob in range(8):
        s, wt, wtok = W.load(wdn(ob), FT * 256)
        for h in range(2):
            mo = ob * 2 + h
            i = C.d_i % 2
            C.d_i += 1
            pd = C.psD[i]
            for kt in range(FT):
                last_mm = P.op("pe", mm(pd[:], wt[:, kt * 256 + h * 128: kt * 256 + h * 128 + 128],
                                        C.hid[:, kt, :], kt == 0, kt == FT - 1),
                               waits=[wtok, hid_last[0], C.relD[i]] if kt == 0 else [])
            last_w = P.op("dve", lambda e, mo=mo, pd=pd: e.tensor_tensor(
                out=xs[:, mo, :], in0=xs[:, mo, :], in1=pd[:], op=ALU.add), waits=[last_mm])
            C.relD[i] = last_w
        W.release(s, last_mm)
    C.hid_free = [last_mm]
    return last_w


def setup_linear_bufs(P, C):
    C.psA = [P.ps("psA%d" % i, [128, NT]) for i in range(2)]
    C.psB = [P.ps("psB%d" % i, [128, NT]) for i in range(2)]
    C.psD = [P.ps("psD%d" % i, [128, NT]) for i in range(2)]
    C.relA = [None, None]
    C.relB = [None, None]
    C.relD = [None, None]
    C.dl_i = 0
    C.d_i = 0
    C.tmp_i = 0
    C.tmpS = [P.sb("tmpS%d" % i, [128, NT], F32) for i in range(2)]
    C.tmpS_free = [None, None]
    C.xh = P.sb("xh", [128, KT, NT], BF16)
    C.xh_free = None
    C.hidflat = P.sb("hid", [128, FT * NT], BF16)
    C.hid = C.hidflat[:, :].rearrange("p (k t) -> p k t", t=NT)
    C.hid_free = []


WROWS = 6272


def build_wcast():
    nc = bass.Bass("TRN2", target_bir_lowering=False)
    x = nc.dram_tensor("w32", [WROWS, 2048], F32, kind="ExternalInput").ap()
    y = nc.dram_tensor("w16", [WROWS, 2048], BF16, kind="ExternalOutput").ap()
    P = Prog(nc)
    ds = P.dsem("st")
    nch = 14
    rr = WROWS // nch
    toks = []
    for i in range(nch):
        toks.append(P.dma("pool", ds, y[i * rr:(i + 1) * rr, :], x[i * rr:(i + 1) * rr, :]))
    P.fence("pool", [toks[-1]])
    P.finish()
    return nc


def tile_w(Wm, mc):
    K, M = Wm.shape
    return np.ascontiguousarray(Wm.reshape(K // 128, 128, M // mc, mc).transpose(2, 1, 0, 3))


def pair_blocks(A, Bm, half):
    K, M = A.shape
    nb = M // half
    out = np.empty((K, nb, 2, half), np.float32)
    out[:, :, 0, :] = A.reshape(K, nb, half)
    out[:, :, 1, :] = Bm.reshape(K, nb, half)
    return out.reshape(K, nb * 2 * half)


WNAMES = ["glu", "gu0", "dn0", "wk", "wv", "wq", "wo", "gu1", "dn1"]
WSHAPES = {
    "glu": (8, 128, 16, 512), "gu0": (22, 128, 16, 512), "dn0": (8, 128, 44, 256),
    "wk": (4, 128, 16, 512), "wv": (4, 128, 16, 512), "wq": (16, 128, 16, 384),
    "wo": (4, 128, 16, 512), "gu1": (22, 128, 16, 512), "dn1": (8, 128, 44, 256),
}


def prep_weights_fp32(inp):
    w = {}
    g = inp["w_glu"][0]
    w["glu"] = tile_w(pair_blocks(g[:, :D], g[:, D:], 256), 512)
    for l in range(2):
        gu = inp["w_gate_up"][l]
        w["gu%d" % l] = tile_w(pair_blocks(gu[:, :DFF], gu[:, DFF:], 256), 512)
        w["dn%d" % l] = tile_w(inp["w_down"][l], 256)
    w["wk"] = tile_w(inp["w_kv"][:, :D], 512)
    w["wv"] = tile_w(inp["w_kv"][:, D:], 512)
    q = inp["w_q"][0].reshape(D, 3, 16, 128).transpose(0, 2, 1, 3).reshape(D, 16 * 384)
    w["wq"] = tile_w(q, 384)
    w["wo"] = tile_w(inp["w_o"][0], 512)
    for k in WNAMES:
        assert w[k].shape == WSHAPES[k], (k, w[k].shape)
    return w


def convert_weights(inp, launches):
    w = prep_weights_fp32(inp)
    flat = np.concatenate([w[k].reshape(-1) for k in WNAMES])
    assert flat.size == NCORES * WROWS * 2048
    flat = flat.reshape(NCORES, WROWS, 2048)
    nc = build_wcast()
    res = run_bass_kernel_spmd(nc, [{"w32": flat[c]} for c in range(NCORES)], core_ids=list(range(NCORES)))
    launches.append("wcast")
    out = np.concatenate([res.results[c]["w16"].reshape(-1) for c in range(NCORES)])
    wb = {}
    o = 0
    for k in WNAMES:
        n = int(np.prod(WSHAPES[k]))
        wb[k] = out[o:o + n].reshape(WSHAPES[k])
        o += n
    return wb


def feat_view(ap2d, t0, n):
    return ap2d.rearrange("(k p) t -> p k t", p=128)[:, :, t0:t0 + n]


def build_stageB(do_glu=True, ntiles=TPC // NT):
    nc = bass.Bass("TRN2", target_bir_lowering=False)
    xT = nc.dram_tensor("xT", [D, TPC], F32, kind="ExternalInput").ap()
    gT = nc.dram_tensor("gT", [D, TPC], BF16, kind="ExternalInput").ap()
    gains = nc.dram_tensor("gains", [128, 64], F32, kind="ExternalInput").ap()
    wd = {k: nc.dram_tensor("w_" + k, list(WSHAPES[k]), BF16, kind="ExternalInput").ap()
          for k in ["glu", "gu0", "dn0", "wk", "wv"]}
    x2T = nc.dram_tensor("x2T", [D, TPC], F32, kind="ExternalOutput").ap()
    kT = nc.dram_tensor("kT", [D, TPC], BF16, kind="ExternalOutput").ap()
    vO = nc.dram_tensor("v", [TPC, D], BF16, kind="ExternalOutput").ap()
    hT = nc.dram_tensor("hT", [D, TPC], BF16, kind="ExternalOutput").ap()

    P = Prog(nc)
    C = Ctx()
    setup_common(P, C)
    setup_linear_bufs(P, C)
    W = WRing(P)
    xs = P.sb("xs", [128, KT, NT], F32)
    gs = P.sb("gs", [128, KT, NT], BF16)
    gn = P.sb("gn", [128, 64], F32)
    s_g = P.dsem("s_g")
    s_x = P.dsem("s_x")
    s_gs = P.dsem("s_gs")
    s_st = P.dsem("s_st")
    s_sth = P.dsem("s_sth")
    s_stk = P.dsem("s_stk")
    s_stv = P.dsem("s_stv")
    t_gn = P.dma("pool", s_g, gn[:], gains)
    C.t_gn = t_gn
    kout = C.hidflat[:, 0:KT * NT].rearrange("p (k t) -> p k t", t=NT)
    vout = C.hidflat[:, KT * NT:2 * KT * NT].rearrange("p (s c) -> p s c", c=D)

    xs_free = []
    gs_free = []
    st_toks = []
    for ti in range(ntiles):
        t0 = ti * NT
        tx = P.dma("pool", s_x, xs[:], feat_view(xT, t0, NT), waits=xs_free)
        xs_tok = tx
        if do_glu:
            tg = P.dma("pool", s_gs, gs[:], feat_view(gT, t0, NT), waits=gs_free)
            last = [None]

            def epi(mo, pa, pb, ta, tb):
                i = C.tmp_i % 2
                C.tmp_i += 1
                tmp = C.tmpS[i]
                t1 = P.op("act", lambda e: e.activation(out=tmp[:], in_=pb[:], func=AF.Sigmoid),
                          waits=[tb, C.tmpS_free[i]])
                t2 = P.op("dve", lambda e: e.tensor_tensor(out=tmp[:], in0=tmp[:], in1=pa[:], op=ALU.mult),
                          waits=[t1, ta])
                t3 = P.op("dve", lambda e: e.tensor_tensor(out=xs[:, mo, :], in0=xs[:, mo, :], in1=tmp[:], op=ALU.add),
                          waits=[t2, tx])
                C.tmpS_free[i] = t3
                last[0] = t3
                return t2, t1

            lm = emit_dual_linear(P, C, W, gs, tg, lambda mb: wd["glu"][mb].rearrange("p k c -> p (k c)"), 8, epi)
            gs_free = [lm]
            xs_tok = last[0]
        xs_tok = emit_ffn(P, C, W, xs, xs_tok, gn, 0,
                          lambda mb: wd["gu0"][mb].rearrange("p k c -> p (k c)"),
                          lambda ob: wd["dn0"][ob].rearrange("p k c -> p (k c)"))
        t_st = P.dma("pool", s_st, feat_view(x2T, t0, NT), xs[:], waits=[xs_tok])
        st_toks = [t_st]
        rstd_tok = emit_sumsq_rstd(P, C, lambda k: xs[:, k, :], KT, xs_tok, 1.0 / D)
        xkv_tok = emit_apply_norm(P, C, xs, gn, 16, gs, rstd_tok, xs_tok, gs_free[0] if gs_free else None)
        h1_tok = emit_apply_norm(P, C, xs, gn, 32, C.xh, rstd_tok, xs_tok, C.xh_free)
        t_h = P.dma("pool", s_sth, feat_view(hT, t0, NT), C.xh[:], waits=[h1_tok])
        C.xh_free = t_h
        xs_free = [t_st, h1_tok]
        pk_banks = [C.psA[0], C.psA[1], C.psB[0], C.psB[1]]
        pk_rel = [C.relA, C.relA, C.relB, C.relB]
        last_mm = None
        k_last = None
        for hb in range(4):
            s, wt, wtok = W.load(wd["wk"][hb].rearrange("p k c -> p (k c)"), KT * 512)
            for h4 in range(4):
                hd = hb * 4 + h4
                bi = hd % 4
                pk = pk_banks[bi]
                rel = pk_rel[bi]
                for kt in range(KT):
                    last_mm = P.op("pe", mm(pk[:], wt[:, kt * 512 + h4 * 128: kt * 512 + h4 * 128 + 128],
                                            gs[:, kt, :], kt == 0, kt == KT - 1),
                                   waits=([wtok, xkv_tok, rel[bi % 2]] + C.hid_free) if kt == 0 else [])
                r_tok = emit_sumsq_rstd(P, C, lambda k, pk=pk: pk[:], 1, last_mm, 1.0 / 128)
                k_last = P.op("dve", lambda e, hd=hd, pk=pk: e.scalar_tensor_tensor(
                    out=kout[:, hd, :], in0=pk[:], scalar=gn[:, 48:49], in1=C.rstd[:],
                    op0=ALU.mult, op1=ALU.mult), waits=[r_tok, t_gn])
                C.rstd_readers.append(k_last)
                rel[bi % 2] = k_last
            W.release(s, last_mm)
        t_k = P.dma("pool", s_stk, feat_view(kT, t0, NT), kout, waits=[k_last])
        v_last = None
        for vb in range(4):
            s, wt, wtok = W.load(wd["wv"][vb].rearrange("p k c -> p (k c)"), KT * 512)
            for ts in range(4):
                i = C.d_i % 2
                C.d_i += 1
                pd = C.psD[i]
                for kt in range(KT):
                    last_mm = P.op("pe", mm(pd[:], gs[:, kt, ts * 128:(ts + 1) * 128],
                                            wt[:, kt * 512:(kt + 1) * 512], kt == 0, kt == KT - 1),
                                   waits=[wtok, xkv_tok, C.relD[i]] if kt == 0 else [])
                v_last = P.op("act", lambda e, ts=ts, vb=vb, pd=pd: e.activation(
                    out=vout[:, ts, vb * 512:(vb + 1) * 512], in_=pd[:], func=AF.Copy), waits=[last_mm] + C.hid_free)
                C.relD[i] = v_last
            W.release(s, last_mm)
        gs_free = [last_mm]
        t_v = P.dma("pool", s_stv, vO[t0:t0 + NT, :].rearrange("(s p) c -> p s c", p=128), vout, waits=[v_last])
        C.hid_free = [t_k, t_v]
        st_toks = [t_st, t_h, t_k, t_v]
    P.fence("pool", st_toks)
    P.finish()
    return nc


def build_stageA0(ntiles=TPC // NT):
    nc = bass.Bass("TRN2", target_bir_lowering=False)
    xT = nc.dram_tensor("xT", [D, TPC], F32, kind="ExternalInput").ap()
    gains = nc.dram_tensor("gains", [128, 64], F32, kind="ExternalInput").ap()
    uT = nc.dram_tensor("uT", [D, TPC], F32, kind="ExternalOutput").ap()
    P = Prog(nc)
    C = Ctx()
    setup_common(P, C)
    xs = P.sb("xs", [128, KT, NT], F32)
    us = P.sb("us", [128, KT, NT], F32)
    gn = P.sb("gn", [128, 64], F32)
    s_g = P.dsem("s_g")
    s_x = P.dsem("s_x")
    s_st = P.dsem("s_st")
    C.t_gn = P.dma("pool", s_g, gn[:], gains)
    xs_free = []
    us_free = None
    t_st = None
    for ti in range(ntiles):
        t0 = ti * NT
        tx = P.dma("pool", s_x, xs[:], feat_view(xT, t0, NT), waits=xs_free)
        rstd_tok = emit_sumsq_rstd(P, C, lambda k: xs[:, k, :], KT, tx, 1.0 / D)
        a_tok = emit_apply_norm(P, C, xs, gn, 0, us, rstd_tok, tx, us_free)
        t_st = P.dma("pool", s_st, feat_view(uT, t0, NT), us[:], waits=[a_tok])
        us_free = t_st
        xs_free = [a_tok]
    P.fence("pool", [t_st])
    P.finish()
    return nc


TC_S5 = 512
NTOK_ALL = BATCH * SEQ
TWO_PI = 6.283185307179586


def build_stageA1(nchunks=SEQ // TC_S5, nbatch=BATCH):
    nc = bass.Bass("TRN2", target_bir_lowering=False)
    uI = nc.dram_tensor("u", [2, 128, NTOK_ALL], F32, kind="ExternalInput").ap()
    spp = nc.dram_tensor("spp", [128, 3, 8], F32, kind="ExternalInput").ap()
    bl = nc.dram_tensor("bl", [128, 2, 8, 16], F32, kind="ExternalInput").ap()
    cl = nc.dram_tensor("cl", [128, 2, 8, 128], F32, kind="ExternalInput").ap()
    dsk = nc.dram_tensor("dsk", [128, 2], F32, kind="ExternalInput").ap()
    idn = nc.dram_tensor("idn", [128, 128], F32, kind="ExternalInput").ap()
    gO = nc.dram_tensor("g", [2, 128, NTOK_ALL], BF16, kind="ExternalOutput").ap()

    P = Prog(nc)
    T = TC_S5
    sp = P.sb("sp", [128, 3, 8], F32)
    bls = P.sb("bls", [128, 2, 8, 16], F32)
    cls = P.sb("cls", [128, 2, 8, 128], F32)
    dss = P.sb("dss", [128, 2], F32)
    ident = P.sb("ident", [128, 128], F32)
    s_in = P.dsem("s_in")
    toks = [P.dma("pool", s_in, sp[:], spp), P.dma("pool", s_in, bls[:], bl),
            P.dma("pool", s_in, cls[:], cl), P.dma("pool", s_in, dss[:], dsk),
            P.dma("pool", s_in, ident[:], idn)]
    prev = [toks[-1]]

    def S(eng, fn):
        t = P.op(eng, fn, waits=prev[:])
        prev[0] = t
        return t

    def t8(name):
        return P.sb(name, [128, 8], F32)

    are, aim, ldt = sp[:, 0, :], sp[:, 1, :], sp[:, 2, :]
    dt = t8("dt"); lre = t8("lre"); lredt = t8("lredt"); rho = t8("rho"); ang = t8("ang")
    S("act", lambda e: e.activation(out=dt[:], in_=ldt, func=AF.Exp))
    S("dve", lambda e: e.tensor_single_scalar(out=lre[:], in_=are, scalar=-1e-4, op=ALU.min))
    S("dve", lambda e: e.tensor_tensor(out=lredt[:], in0=lre[:], in1=dt[:], op=ALU.mult))
    S("act", lambda e: e.activation(out=rho[:], in_=lredt[:], func=AF.Exp))
    S("dve", lambda e: e.tensor_tensor(out=ang[:], in0=aim, in1=dt[:], op=ALU.mult))

    def sincos(out, off, nm):
        f = t8("f_" + nm); ki = P.sb("ki_" + nm, [128, 8], I32); kf = t8("kf_" + nm); r = t8("r_" + nm); m = t8("m_" + nm)
        S("dve", lambda e: e.tensor_scalar(out=f[:], in0=ang[:], scalar1=1.0 / TWO_PI, scalar2=16.0 + off,
                                           op0=ALU.mult, op1=ALU.add))
        S("dve", lambda e: e.tensor_copy(out=ki[:], in_=f[:]))
        S("dve", lambda e: e.tensor_copy(out=kf[:], in_=ki[:]))
        S("dve", lambda e: e.tensor_tensor(out=r[:], in0=f[:], in1=kf[:], op=ALU.subtract))
        S("dve", lambda e: e.tensor_single_scalar(out=m[:], in_=r[:], scalar=0.0, op=ALU.is_lt))
        S("dve", lambda e: e.tensor_tensor(out=r[:], in0=r[:], in1=m[:], op=ALU.add))
        S("dve", lambda e: e.tensor_scalar(out=r[:], in0=r[:], scalar1=TWO_PI - 2e-6, scalar2=-(TWO_PI / 2 - 1e-6),
                                           op0=ALU.mult, op1=ALU.add))
        S("act", lambda e: e.activation(out=out[:], in_=r[:], func=AF.Sin))

    sn = t8("sn"); cs = t8("cs")
    sincos(sn, 0.5, "s")
    sincos(cs, 0.75, "c")
    lbr = t8("lbr"); lbi = t8("lbi"); den = t8("den"); t_a = t8("t_a"); t_b = t8("t_b")
    nr = t8("nr"); cre = t8("cre"); cim = t8("cim"); ncim = t8("ncim")
    S("dve", lambda e: e.tensor_tensor(out=lbr[:], in0=rho[:], in1=cs[:], op=ALU.mult))
    S("dve", lambda e: e.tensor_tensor(out=lbi[:], in0=rho[:], in1=sn[:], op=ALU.mult))
    S("dve", lambda e: e.tensor_tensor(out=den[:], in0=lre[:], in1=lre[:], op=ALU.mult))
    S("dve", lambda e: e.tensor_tensor(out=t_a[:], in0=aim, in1=aim, op=ALU.mult))
    S("dve", lambda e: e.tensor_tensor(out=den[:], in0=den[:], in1=t_a[:], op=ALU.add))
    S("dve", lambda e: e.reciprocal(out=den[:], in_=den[:]))
    S("dve", lambda e: e.tensor_single_scalar(out=nr[:], in_=lbr[:], scalar=-1.0, op=ALU.add))
    S("dve", lambda e: e.tensor_tensor(out=t_a[:], in0=nr[:], in1=lre[:], op=ALU.mult))
    S("dve", lambda e: e.tensor_tensor(out=t_b[:], in0=lbi[:], in1=aim, op=ALU.mult))
    S("dve", lambda e: e.tensor_tensor(out=t_a[:], in0=t_a[:], in1=t_b[:], op=ALU.add))
    S("dve", lambda e: e.tensor_tensor(out=cre[:], in0=t_a[:], in1=den[:], op=ALU.mult))
    S("dve", lambda e: e.tensor_tensor(out=t_a[:], in0=lbi[:], in1=lre[:], op=ALU.mult))
    S("dve", lambda e: e.tensor_tensor(out=t_b[:], in0=nr[:], in1=aim, op=ALU.mult))
    S("dve", lambda e: e.tensor_tensor(out=t_a[:], in0=t_a[:], in1=t_b[:], op=ALU.subtract))
    S("dve", lambda e: e.tensor_tensor(out=cim[:], in0=t_a[:], in1=den[:], op=ALU.mult))
    S("dve", lambda e: e.tensor_single_scalar(out=ncim[:], in_=cim[:], scalar=-1.0, op=ALU.mult))

    PT = P.sb("PT", [128, 8, 2, 128], F32)
    tb1 = P.sb("tb1", [128, 16], F32)
    S("dve", lambda e: e.memset(PT[:], 0.0))
    for k in range(8):
        for g2 in range(2):
            ps_ = slice(g2 * 64, (g2 + 1) * 64)
            c0 = ((2 * k + g2) % 8) * 16
            S("dve", lambda e, k=k, ps_=ps_: e.tensor_scalar(out=tb1[ps_, :], in0=bls[ps_, 0, k, :], scalar1=cre[ps_, k:k + 1],
                                                           scalar2=None, op0=ALU.mult))
            S("dve", lambda e, k=k, ps_=ps_, c0=c0: e.scalar_tensor_tensor(
                out=PT[ps_, k, 0, c0:c0 + 16], in0=bls[ps_, 1, k, :], scalar=ncim[ps_, k:k + 1], in1=tb1[ps_, :],
                op0=ALU.mult, op1=ALU.add))
            S("dve", lambda e, k=k, ps_=ps_: e.tensor_scalar(out=tb1[ps_, :], in0=bls[ps_, 1, k, :], scalar1=cre[ps_, k:k + 1],
                                                           scalar2=None, op0=ALU.mult))
            S("dve", lambda e, k=k, ps_=ps_, c0=c0: e.scalar_tensor_tensor(
                out=PT[ps_, k, 1, c0:c0 + 16], in0=bls[ps_, 0, k, :], scalar=cim[ps_, k:k + 1], in1=tb1[ps_, :],
                op0=ALU.mult, op1=ALU.add))
    BT = P.sb("BT", [128, 8, 2, 128], BF16)
    ps_t = P.ps("ps_t", [128, 512])
    for k in range(8):
        for ri in range(2):
            S("pe", lambda e, k=k, ri=ri: e.transpose(out=ps_t[:, 0:128], in_=PT[:, k, ri, :], identity=ident[:]))
            S("dve", lambda e, k=k, ri=ri: e.tensor_copy(out=BT[:, k, ri, :], in_=ps_t[:, 0:128]))
    CT = P.sb("CT", [128, 2, 8, 128], BF16)
    S("dve", lambda e: e.tensor_copy(out=CT[:, 0, :, :], in_=cls[:, 0, :, :]))
    S("act", lambda e: e.activation(out=CT[:, 1, :, :], in_=cls[:, 1, :, :], func=AF.Copy, scale=-1.0))

    TCs = P.sb("TCs", [128, 8, T], F32)
    TSn = P.sb("TSn", [128, 8, T], F32)
    RHO = P.sb("RHO", [128, 8, T], F32)
    er = t8("er"); ei = t8("ei"); e2 = t8("e2"); ner = t8("nei")
    S("dve", lambda e: e.tensor_copy(out=er[:], in_=cs[:]))
    S("dve", lambda e: e.tensor_copy(out=ei[:], in_=sn[:]))
    S("dve", lambda e: e.memset(TCs[:, :, 0:1], 1.0))
    S("dve", lambda e: e.memset(TSn[:, :, 0:1], 0.0))
    tw = P.sb("tw", [128, T], F32)
    m = 1
    while m < T:
        S("dve", lambda e: e.tensor_single_scalar(out=ner[:], in_=ei[:], scalar=-1.0, op=ALU.mult))
        for k in range(8):
            S("dve", lambda e, k=k, m=m: e.tensor_scalar(out=tw[:, 0:m], in0=TCs[:, k, 0:m], scalar1=er[:, k:k + 1],
                                                         scalar2=None, op0=ALU.mult))
            S("dve", lambda e, k=k, m=m: e.scalar_tensor_tensor(out=TCs[:, k, m:2 * m], in0=TSn[:, k, 0:m], scalar=ner[:, k:k + 1],
                                                                in1=tw[:, 0:m], op0=ALU.mult, op1=ALU.add))
            S("dve", lambda e, k=k, m=m: e.tensor_scalar(out=tw[:, 0:m], in0=TSn[:, k, 0:m], scalar1=er[:, k:k + 1],
                                                         scalar2=None, op0=ALU.mult))
            S("dve", lambda e, k=k, m=m: e.scalar_tensor_tensor(out=TSn[:, k, m:2 * m], in0=TCs[:, k, 0:m], scalar=ei[:, k:k + 1],
                                                                in1=tw[:, 0:m], op0=ALU.mult, op1=ALU.add))
        S("dve", lambda e: e.tensor_tensor(out=e2[:], in0=er[:], in1=ei[:], op=ALU.mult))
        S("dve", lambda e: e.tensor_tensor(out=er[:], in0=er[:], in1=er[:], op=ALU.mult))
        S("dve", lambda e: e.tensor_tensor(out=ei[:], in0=ei[:], in1=ei[:], op=ALU.mult))
        S("dve", lambda e: e.tensor_tensor(out=er[:], in0=er[:], in1=ei[:], op=ALU.subtract))
        S("dve", lambda e: e.tensor_single_scalar(out=ei[:], in_=e2[:], scalar=2.0, op=ALU.mult))
        m *= 2
    S("dve", lambda e: e.tensor_single_scalar(out=ner[:], in_=ei[:], scalar=-1.0, op=ALU.mult))
    for k in range(8):
        S("dve", lambda e, k=k: e.tensor_scalar(out=RHO[:, k, :], in0=TCs[:, k, :], scalar1=0.0, scalar2=rho[:, k:k + 1],
                                                op0=ALU.mult, op1=ALU.add))
    t_pre = prev[0]

    us32 = [P.sb("us32_%d" % i, [128, 2, T], F32) for i in range(2)]
    usb = [P.sb("usb_%d" % i, [128, 2, T], BF16) for i in range(2)]
    s_u = [P.dsem("s_u%d" % i) for i in range(2)]
    us_free = [[], []]
    zr = [P.ps("zr%d" % i, [128, T]) for i in range(2)]
    zi = [P.ps("zi%d" % i, [128, T]) for i in range(2)]
    z_free = [None, None]
    yps = [P.ps("yps%d" % i, [128, T]) for i in range(2)]
    y_free = [None, None]
    wa = P.sb("wa", [128, T], F32); wb_ = P.sb("wb", [128, T], F32)
    gir = P.sb("gir", [128, T], F32); gii = P.sb("gii", [128, T], F32)
    GR = P.sb("GR", [128, T], F32); GI = P.sb("GI", [128, T], F32)
    HR = [P.sb("HR%d" % i, [128, T], BF16) for i in range(2)]
    HI = [P.sb("HI%d" % i, [128, T], BF16) for i in range(2)]
    h_free = [None, None]
    inr = P.sb("inr", [128, 8], F32); ini = P.sb("ini", [128, 8], F32)
    tc1 = P.sb("tc1", [128, 1], F32)
    yv = P.sb("yv", [128, T], F32)
    gout = [P.sb("gout%d" % i, [128, 2, T], BF16) for i in range(2)]
    s_go = [P.dsem("s_go%d" % i) for i in range(2)]
    go_free = [None, None]
    it = 0
    vprev = [None]

    def V(fn, waits=()):
        t = P.op("dve", fn, waits=list(waits) + [vprev[0]])
        vprev[0] = t
        return t
    zi_i = 0
    y_i = 0
    st_last = []
    for b in range(nbatch):
        t_z = V(lambda e: e.memset(inr[:], 0.0), waits=[t_pre])
        t_z = V(lambda e: e.memset(ini[:], 0.0), waits=[t_pre])
        for ch in range(nchunks):
            t0 = b * SEQ + ch * T
            ui = it % 2
            it += 1
            tu = P.dma("sp", s_u[ui], us32[ui][:], uI[:, :, t0:t0 + T].rearrange("k p t -> p k t"), waits=us_free[ui])
            tcast = P.op("act", lambda e, ui=ui: e.activation(out=usb[ui][:], in_=us32[ui][:], func=AF.Copy), waits=[tu])
            ylast = [None, None]
            for k in range(8):
                kt = k // 4
                zb = zi_i % 2
                zi_i += 1
                tzr = P.op("pe", mm(zr[zb][:], BT[:, k, 0, :], usb[ui][:, kt, :], True, True), waits=[tcast, z_free[zb], t_pre])
                tzi = P.op("pe", mm(zi[zb][:], BT[:, k, 1, :], usb[ui][:, kt, :], True, True))
                V(lambda e, k=k, zb=zb: e.tensor_tensor(out=wa[:], in0=zr[zb][:], in1=TCs[:, k, :], op=ALU.mult), waits=[tzr, tzi])
                V(lambda e, k=k, zb=zb: e.tensor_tensor(out=wb_[:], in0=zi[zb][:], in1=TSn[:, k, :], op=ALU.mult))
                V(lambda e: e.tensor_tensor(out=gir[:], in0=wa[:], in1=wb_[:], op=ALU.add))
                V(lambda e, k=k, zb=zb: e.tensor_tensor(out=wa[:], in0=zi[zb][:], in1=TCs[:, k, :], op=ALU.mult))
                tz2 = V(lambda e, k=k, zb=zb: e.tensor_tensor(out=wb_[:], in0=zr[zb][:], in1=TSn[:, k, :], op=ALU.mult))
                z_free[zb] = tz2
                V(lambda e: e.tensor_tensor(out=gii[:], in0=wa[:], in1=wb_[:], op=ALU.subtract))
                V(lambda e, k=k: e.tensor_tensor_scan(out=GR[:], data0=RHO[:, k, :], data1=gir[:],
                                                                initial=inr[:, k:k + 1], op0=ALU.mult, op1=ALU.add))
                V(lambda e, k=k: e.tensor_tensor_scan(out=GI[:], data0=RHO[:, k, :], data1=gii[:],
                                                                initial=ini[:, k:k + 1], op0=ALU.mult, op1=ALU.add))
                V(lambda e, k=k: e.tensor_scalar(out=tc1[:], in0=GR[:, T - 1:T], scalar1=er[:, k:k + 1], scalar2=None, op0=ALU.mult))
                V(lambda e, k=k: e.scalar_tensor_tensor(out=inr[:, k:k + 1], in0=GI[:, T - 1:T], scalar=ner[:, k:k + 1],
                                                                  in1=tc1[:], op0=ALU.mult, op1=ALU.add))
                V(lambda e, k=k: e.tensor_scalar(out=tc1[:], in0=GI[:, T - 1:T], scalar1=er[:, k:k + 1], scalar2=None, op0=ALU.mult))
                V(lambda e, k=k: e.scalar_tensor_tensor(out=ini[:, k:k + 1], in0=GR[:, T - 1:T], scalar=ei[:, k:k + 1],
                                                                  in1=tc1[:], op0=ALU.mult, op1=ALU.add))
                hb = zb
                V(lambda e, k=k: e.tensor_tensor(out=wa[:], in0=GR[:], in1=TCs[:, k, :], op=ALU.mult))
                V(lambda e, k=k: e.tensor_tensor(out=wb_[:], in0=GI[:], in1=TSn[:, k, :], op=ALU.mult))
                V(lambda e, hb=hb: e.tensor_tensor(out=HR[hb][:], in0=wa[:], in1=wb_[:], op=ALU.subtract), waits=[h_free[hb]])
                V(lambda e, k=k: e.tensor_tensor(out=wa[:], in0=GR[:], in1=TSn[:, k, :], op=ALU.mult))
                V(lambda e, k=k: e.tensor_tensor(out=wb_[:], in0=GI[:], in1=TCs[:, k, :], op=ALU.mult))
                th = V(lambda e, hb=hb: e.tensor_tensor(out=HI[hb][:], in0=wa[:], in1=wb_[:], op=ALU.add))
                yb = y_i % 2
                k4 = k % 4
                P.op("pe", mm(yps[yb][:], CT[:, 0, k, :], HR[hb][:], k4 == 0, False), waits=[th] + ([y_free[yb]] if k4 == 0 else []))
                ty = P.op("pe", mm(yps[yb][:], CT[:, 1, k, :], HI[hb][:], False, k4 == 3))
                h_free[hb] = ty
                if k4 == 3:
                    y_i += 1
                    t1 = V(lambda e, kt=kt, ui=ui, yb=yb: e.scalar_tensor_tensor(
                        out=yv[:], in0=us32[ui][:, kt, :], scalar=dss[:, kt:kt + 1], in1=yps[yb][:],
                        op0=ALU.mult, op1=ALU.add), waits=[ty, tu] + ([ylast[0]] if ylast[0] else []))
                    y_free[yb] = t1
                    t2 = P.op("act", lambda e, kt=kt, ui=ui: e.activation(out=gout[ui][:, kt, :], in_=yv[:], func=AF.Gelu_apprx_tanh),
                              waits=[t1, go_free[ui]])
                    ylast[0] = t2
            tst = P.dma("pool", s_go[ui], gO[:, :, t0:t0 + T].rearrange("k p t -> p k t"), gout[ui][:], waits=[ylast[0]])
            go_free[ui] = tst
            us_free[ui] = [ylast[0]]
            st_last = [go_free[0], go_free[1]]
    P.fence("pool", st_last)
    P.finish()
    return nc


def s5_inmaps(inp, u_cm):
    are, aim, ldt = inp["ssm_a_re"][0], inp["ssm_a_im"][0], inp["ssm_log_dt"][0]
    bre, bim = inp["ssm_b_re"][0], inp["ssm_b_im"][0]
    cre, cim = inp["ssm_c_re"][0], inp["ssm_c_im"][0]
    dsk = inp["ssm_d"][0]
    idn = np.eye(128, dtype=np.float32)
    maps = []
    for c in range(NCORES):
        G0 = 16 * c
        spp = np.zeros((128, 3, 8), np.float32)
        bl = np.zeros((128, 2, 8, 16), np.float32)
        cl = np.zeros((128, 2, 8, 128), np.float32)
        for k in range(8):
            for g2 in range(2):
                g = G0 + 2 * k + g2
                rows = slice(g2 * 64, (g2 + 1) * 64)
                spp[rows, 0, k] = are[g]
                spp[rows, 1, k] = aim[g]
                spp[rows, 2, k] = ldt[g]
                bl[rows, 0, k, :] = bre[g]
                bl[rows, 1, k, :] = bim[g]
                m0 = ((2 * k + g2) % 8) * 16
                cl[rows, 0, k, m0:m0 + 16] = cre[g].T
                cl[rows, 1, k, m0:m0 + 16] = cim[g].T
        dd = np.ascontiguousarray(dsk[256 * c:256 * (c + 1)].reshape(2, 128).T)
        uc = np.ascontiguousarray(u_cm[256 * c:256 * (c + 1)].reshape(2, 128, -1))
        maps.append({"u": uc, "spp": spp, "bl": bl, "cl": cl, "dsk": dd, "idn": idn})
    return maps


SPAN = 2048
DIL = (1, 4, 16)
NSPAN = TPC // SPAN
KCOLS = TPC + SPAN


def perm_index(d):
    S = 128 * d
    pos = np.arange(SPAN)
    n, rem = pos // S, pos % S
    r, i = rem // 128, rem % 128
    return n * S + i * d + r


def attn_tables():
    k = np.arange(128)[:, None].astype(np.float64)
    q = np.arange(128)[None, :].astype(np.float64)
    out = np.zeros((16, 128, 3, 256), np.float32)
    for hd in range(16):
        for g in range(3):
            slope = 2.0 ** (-8.0 * (3 * hd + g + 1) / 48.0)
            d = DIL[g]
            dist_c = q - k
            wc = np.where(dist_c >= 0, np.exp(-slope * d * dist_c), 0.0)
            dist_p = q + 128 - k
            wp = np.where(dist_p <= 128, np.exp(-slope * d * dist_p), 0.0)
            out[hd, :, g, 0:128] = wp
            out[hd, :, g, 128:256] = wc
    return out.astype(NPBF)


def build_stageC(nspan=NSPAN, nheads=16):
    nc = bass.Bass("TRN2", target_bir_lowering=False)
    hT = nc.dram_tensor("hT", [D, TPC], BF16, kind="ExternalInput").ap()
    KP = nc.dram_tensor("KP", [3, D, KCOLS], BF16, kind="ExternalInput").ap()
    VP = nc.dram_tensor("VP", [3, 16, 128, KCOLS // 128, 128], BF16, kind="ExternalInput").ap()
    wq = nc.dram_tensor("w_wq", list(WSHAPES["wq"]), BF16, kind="ExternalInput").ap()
    gq = nc.dram_tensor("gq", [128, 4], F32, kind="ExternalInput").ap()
    wtab = nc.dram_tensor("wtab", [16, 128, 3, 256], BF16, kind="ExternalInput").ap()
    oT = nc.dram_tensor("oT", [D, TPC], BF16, kind="ExternalOutput").ap()

    P = Prog(nc)
    C = Ctx()
    setup_common(P, C)
    W = WRing(P, nslots=2, nelem=KT * 384)
    hs = P.sb("hs", [128, KT, SPAN], BF16)
    gqs = P.sb("gqs", [128, 4], F32)
    s_g = P.dsem("s_g")
    C.t_gn = P.dma("pool", s_g, gqs[:], gq)
    s_h = P.dsem("s_h")
    kb = [P.sb("kb%d" % i, [128, 2 * SPAN], BF16) for i in range(2)]
    vb = [P.sb("vb%d" % i, [128, 32, 128], BF16) for i in range(2)]
    s_k = [P.dsem("s_k%d" % i) for i in range(2)]
    s_v = [P.dsem("s_v%d" % i) for i in range(2)]
    kv_free = [None, None]
    wt_s = [P.sb("wt_s%d" % i, [128, 3, 256], BF16) for i in range(2)]
    s_wt = [P.dsem("s_wt%d" % i) for i in range(2)]
    wt_free = [None, None]
    qn = P.sb("qn", [128, 3, SPAN], BF16)
    qn_free = [None, None, None]
    acc = P.sb("acc", [128, 2, SPAN], F32)
    rden = P.sb("rden", [128, SPAN], F32)
    ob = [P.sb("ob%d" % i, [128, SPAN], BF16) for i in range(2)]
    s_o = [P.dsem("s_o%d" % i) for i in range(2)]
    ob_free = [None, None]
    psq = [P.ps("psq%d" % i, [128, NT]) for i in range(2)]
    psq_free = [None, None]
    pss = [P.ps("pss%d" % i, [128, NT]) for i in range(2)]
    pss_free = [None, None]
    pso = [P.ps("pso%d" % i, [128, NT]) for i in range(2)]
    pso_free = [None, None]
    E = [P.sb("E%d" % i, [128, 256], F32) for i in range(2)]
    Pb = [P.sb("Pb%d" % i, [128, 256], BF16) for i in range(2)]
    e_free = [None, None]
    p_free = [None, None]
    SCALE = 128.0 ** -0.5
    q_i = 0
    s_i = 0
    o_i = 0
    it = 0
    kvi = 0
    hs_free = []
    acc_last = None
    fin = []
    for sp in range(nspan):
        th = P.dma("pool", s_h, hs[:], feat_view(hT, sp * SPAN, SPAN), waits=hs_free)
        last_q_mm = None
        for hd in range(nheads):
            ws, wt, wtok = W.load(wq[hd].rearrange("p k c -> p (k c)"), KT * 384)
            ti = it % 2
            it += 1
            ttab = P.dma("sp", s_wt[ti], wt_s[ti][:], wtab[hd], waits=[wt_free[ti]])
            acc_tok = None
            for g in range(3):
                d = DIL[g]
                S = 128 * d
                bi = kvi % 2
                kvi += 1
                tk = P.dma("sp", s_k[bi], kb[bi][:], KP[g, hd * 128:(hd + 1) * 128, sp * SPAN: sp * SPAN + 2 * SPAN],
                           waits=[kv_free[bi]])
                tv = P.dma("sp", s_v[bi], vb[bi][:], VP[g, hd, :, sp * 16: sp * 16 + 32, :], waits=[kv_free[bi]])
                qn_tok = None
                for c4 in range(4):
                    pi = q_i % 2
                    q_i += 1
                    pq = psq[pi]
                    for kt in range(KT):
                        if d == 1:
                            rhs = hs[:, kt, c4 * 512:(c4 + 1) * 512]
                            out = pq[:]
                        elif d == 4:
                            rhs = hs[:, kt, c4 * 512:(c4 + 1) * 512].rearrange("p (i r) -> p r i", r=4)
                            out = pq[:].rearrange("p (r i) -> p r i", r=4)
                        else:
                            rhs = hs[:, kt, :].rearrange("p (i r) -> p r i", r=16)[:, 4 * c4:4 * c4 + 4, :]
                            out = pq[:].rearrange("p (r i) -> p r i", r=4)
                        last_q_mm = P.op("pe", mm(out, wt[:, kt * 384 + g * 128: kt * 384 + g * 128 + 128], rhs,
                                                  kt == 0, kt == KT - 1),
                                         waits=[wtok, th, psq_free[pi]] if kt == 0 else [])
                    r_tok = emit_sumsq_rstd(P, C, lambda k, pq=pq: pq[:], 1, last_q_mm, 1.0 / 128)
                    qn_tok = P.op("dve", lambda e, g=g, c4=c4, pq=pq: e.scalar_tensor_tensor(
                        out=qn[:, g, c4 * 512:(c4 + 1) * 512], in0=pq[:], scalar=gqs[:, g:g + 1], in1=C.rstd[:],
                        op0=ALU.mult, op1=ALU.mult), waits=[r_tok, C.t_gn, qn_free[g]])
                    C.rstd_readers.append(qn_tok)
                    psq_free[pi] = qn_tok
                def emit_scores(blk):
                    nonlocal s_i
                    si = s_i % 2
                    s_i += 1
                    qo = blk * 128
                    ps_ = pss[si]
                    P.op("pe", mm(ps_[:, 0:128], kb[bi][:, SPAN + qo - S: SPAN + qo - S + 128], qn[:, g, qo:qo + 128], True, True),
                         waits=[tk, qn_tok, pss_free[si]])
                    t = P.op("pe", mm(ps_[:, 128:256], kb[bi][:, SPAN + qo: SPAN + qo + 128], qn[:, g, qo:qo + 128], True, True))
                    return si, t

                nxt = emit_scores(0)
                last_pv = None
                last_s = None
                for blk in range(16):
                    si, ts = nxt
                    last_s = ts
                    if blk + 1 < 16:
                        nxt = emit_scores(blk + 1)
                    qo = blk * 128
                    halo = (sp == 0 and qo < S)
                    ps_ = pss[si]
                    ei = si
                    if halo:
                        P.op("act", lambda e, ps_=ps_, ei=ei: e.activation(out=E[ei][:, 0:128], in_=ps_[:, 0:128], func=AF.Exp,
                                                                        bias=gqs[:, 3:4], scale=SCALE), waits=[ts, e_free[ei], C.t_gn])
                        te = P.op("act", lambda e, ps_=ps_, ei=ei: e.activation(out=E[ei][:, 128:256], in_=ps_[:, 128:256], func=AF.Exp,
                                                                             scale=SCALE))
                    else:
                        te = P.op("act", lambda e, ps_=ps_, ei=ei: e.activation(out=E[ei][:], in_=ps_[:, 0:256], func=AF.Exp, scale=SCALE),
                                  waits=[ts, e_free[ei]])
                    pss_free[si] = te
                    tp = P.op("dve", lambda e, ei=ei, ti=ti, g=g: e.tensor_tensor(out=Pb[ei][:], in0=E[ei][:], in1=wt_s[ti][:, g, :], op=ALU.mult),
                              waits=[te, ttab, p_free[ei]])
                    e_free[ei] = tp
                    oi = o_i % 2
                    o_i += 1
                    po = pso[oi]
                    kb_c = (SPAN + qo) // 128
                    kb_p = (SPAN + qo - S) // 128
                    P.op("pe", mm(po[:, 0:128], vb[bi][:, kb_p, :], Pb[ei][:, 0:128], True, False), waits=[tp, tv, pso_free[oi]])
                    P.op("pe", mm(po[:, 0:128], vb[bi][:, kb_c, :], Pb[ei][:, 128:256], False, True))
                    P.op("pe", mm(po[:, 128:256], C.ones_bf[:], Pb[ei][:, 0:128], True, False))
                    last_pv = P.op("pe", mm(po[:, 128:256], C.ones_bf[:], Pb[ei][:, 128:256], False, True))
                    p_free[ei] = last_pv
                    n = qo // S
                    r = (qo % S) // 128
                    if d == 1:
                        acc_ap = acc[:, :, qo:qo + 128]
                    else:
                        acc_ap = acc[:, :, n * S:(n + 1) * S].rearrange("p j (i r) -> p j r i", r=d)[:, :, r, :]
                    src = po[:, 0:256].rearrange("p (j i) -> p j i", j=2)
                    if g == 0:
                        acc_tok = P.op("dve", lambda e, acc_ap=acc_ap, src=src: e.tensor_copy(out=acc_ap, in_=src),
                                       waits=[last_pv, acc_last])
                    else:
                        acc_tok = P.op("dve", lambda e, acc_ap=acc_ap, src=src: e.tensor_tensor(out=acc_ap, in0=acc_ap, in1=src, op=ALU.add),
                                       waits=[last_pv, acc_tok])
                    pso_free[oi] = acc_tok
                kv_free[bi] = last_pv
                qn_free[g] = last_s
            W.release(ws, last_q_mm)
            wt_free[ti] = acc_tok
            oi2 = it % 2
            t1 = P.op("dve", lambda e: e.reciprocal(out=rden[:], in_=acc[:, 1, :]), waits=[acc_tok])
            t2 = P.op("dve", lambda e, oi2=oi2: e.tensor_tensor(out=ob[oi2][:], in0=acc[:, 0, :], in1=rden[:], op=ALU.mult),
                      waits=[t1, ob_free[oi2]])
            acc_last = t2
            to = P.dma("pool", s_o[oi2], oT[hd * 128:(hd + 1) * 128, sp * SPAN:(sp + 1) * SPAN], ob[oi2][:], waits=[t2])
            ob_free[oi2] = to
            fin = [ob_free[0], ob_free[1]]
        hs_free = [last_q_mm]
    P.fence("pool", fin)
    P.finish()
    return nc


def build_stageD(ntiles=TPC // NT):
    nc = bass.Bass("TRN2", target_bir_lowering=False)
    xT = nc.dram_tensor("xT", [D, TPC], F32, kind="ExternalInput").ap()
    oT = nc.dram_tensor("oT", [D, TPC], BF16, kind="ExternalInput").ap()
    gains = nc.dram_tensor("gains", [128, 64], F32, kind="ExternalInput").ap()
    wd = {k: nc.dram_tensor("w_" + k, list(WSHAPES[k]), BF16, kind="ExternalInput").ap() for k in ["wo", "gu1", "dn1"]}
    yT = nc.dram_tensor("yT", [D, TPC], F32, kind="ExternalOutput").ap()
    P = Prog(nc)
    C = Ctx()
    setup_common(P, C)
    setup_linear_bufs(P, C)
    W = WRing(P)
    xs = P.sb("xs", [128, KT, NT], F32)
    os_ = P.sb("os", [128, KT, NT], BF16)
    gn = P.sb("gn", [128, 64], F32)
    s_g = P.dsem("s_g"); s_x = P.dsem("s_x"); s_o = P.dsem("s_o"); s_st = P.dsem("s_st")
    C.t_gn = P.dma("pool", s_g, gn[:], gains)
    xs_free = []
    os_free = []
    t_st = None
    for ti in range(ntiles):
        t0 = ti * NT
        tx = P.dma("pool", s_x, xs[:], feat_view(xT, t0, NT), waits=xs_free)
        to = P.dma("pool", s_o, os_[:], feat_view(oT, t0, NT), waits=os_free)
        last_w = None
        last_mm = None
        for ob in range(4):
            s, wt, wtok = W.load(wd["wo"][ob].rearrange("p k c -> p (k c)"), KT * 512)
            for h in range(4):
                mo = ob * 4 + h
                i = C.d_i % 2
                C.d_i += 1
                pd = C.psD[i]
                for kt in range(KT):
                    last_mm = P.op("pe", mm(pd[:], wt[:, kt * 512 + h * 128: kt * 512 + h * 128 + 128], os_[:, kt, :],
                                            kt == 0, kt == KT - 1), waits=[wtok, to, C.relD[i]] if kt == 0 else [])
                last_w = P.op("dve", lambda e, mo=mo, pd=pd: e.tensor_tensor(out=xs[:, mo, :], in0=xs[:, mo, :], in1=pd[:], op=ALU.add),
                              waits=[last_mm, tx])
                C.relD[i] = last_w
            W.release(s, last_mm)
        os_free = [last_mm]
        xs_tok = emit_ffn(P, C, W, xs, last_w, gn, 0,
                          lambda mb: wd["gu1"][mb].rearrange("p k c -> p (k c)"),
                          lambda ob: wd["dn1"][ob].rearrange("p k c -> p (k c)"))
        t_st = P.dma("pool", s_st, feat_view(yT, t0, NT), xs[:], waits=[xs_tok])
        xs_free = [t_st]
    P.fence("pool", [t_st])
    P.finish()
    return nc


DBG = None


def _lay(vec):
    return np.ascontiguousarray(np.asarray(vec, np.float32).reshape(-1, 128).T)


def _run(nc, in_maps, launches, name):
    res = run_bass_kernel_spmd(nc, in_maps, core_ids=list(range(NCORES)))
    launches.append(name)
    return res.results


def kernel(**inp):
    inp = {k: np.asarray(v) for k, v in inp.items()}
    launches = []
    x = inp["x"]
    wb = convert_weights(inp, launches)
    xT_c = []
    for c in range(NCORES):
        b, j = c // 4, c % 4
        xT_c.append(np.ascontiguousarray(x[b, j * TPC:(j + 1) * TPC, :].T))
    gnA = np.zeros((128, 64), np.float32)
    gnA[:, 0:16] = _lay(inp["mix_norm"][0])
    r = _run(build_stageA0(), [{"xT": xT_c[c], "gains": gnA} for c in range(NCORES)], launches, "A0")
    u_cm = np.concatenate([r[c]["uT"] for c in range(NCORES)], axis=1)
    r = _run(build_stageA1(), s5_inmaps(inp, u_cm), launches, "A1")
    g_cm = np.concatenate([r[c]["g"].reshape(256, NTOK_ALL) for c in range(NCORES)], axis=0)
    del u_cm
    gnB = np.zeros((128, 64), np.float32)
    gnB[:, 0:16] = _lay(inp["ffn_norm"][0])
    gnB[:, 16:32] = _lay(inp["kv_norm"])
    gnB[:, 32:48] = _lay(inp["mix_norm"][1])
    gnB[:, 48] = inp["k_norm"]
    maps = []
    for c in range(NCORES):
        m = {"xT": xT_c[c], "gT": np.ascontiguousarray(g_cm[:, c * TPC:(c + 1) * TPC]), "gains": gnB}
        for k in ["glu", "gu0", "dn0", "wk", "wv"]:
            m["w_" + k] = wb[k]
        maps.append(m)
    rB = _run(build_stageB(), maps, launches, "B")
    del g_cm, maps
    if DBG is not None:
        DBG["x2T"] = [rB[c]["x2T"] for c in range(NCORES)]
        DBG["kT"] = [rB[c]["kT"] for c in range(NCORES)]
        DBG["v"] = [rB[c]["v"] for c in range(NCORES)]
        DBG["hT"] = [rB[c]["hT"] for c in range(NCORES)]
    wtab = attn_tables()
    perms = [perm_index(d) for d in DIL]
    maps = []
    for b in range(BATCH):
        kseq = np.concatenate([rB[b * 4 + j]["kT"] for j in range(4)], axis=1)
        vseq = np.concatenate([rB[b * 4 + j]["v"] for j in range(4)], axis=0)
        kps, vps = [], []
        for g in range(3):
            idx = (np.arange(0, SEQ, SPAN)[:, None] + perms[g][None, :]).reshape(-1)
            kp = np.concatenate([np.zeros((D, SPAN), NPBF), kseq[:, idx]], axis=1)
            vp = np.concatenate([np.zeros((SPAN, D), NPBF), vseq[idx, :]], axis=0)
            kps.append(kp)
            vps.append(vp)
        for j in range(4):
            KPc = np.stack([kps[g][:, j * TPC: j * TPC + KCOLS] for g in range(3)], axis=0)
            VPc = np.stack([vps[g][j * TPC: j * TPC + KCOLS, :].reshape(KCOLS // 128, 128, 16, 128).transpose(2, 1, 0, 3)
                            for g in range(3)], axis=0)
            gq = np.zeros((128, 4), np.float32)
            gq[:, 0:3] = inp["q_norm"][0].T
            gq[:, 3] = -30000.0 if j == 0 else 0.0
            maps.append({"hT": rB[b * 4 + j]["hT"], "KP": np.ascontiguousarray(KPc), "VP": np.ascontiguousarray(VPc),
                         "w_wq": wb["wq"], "gq": gq, "wtab": wtab})
        del kseq, vseq, kps, vps
    rC = _run(build_stageC(), maps, launches, "C")
    del maps
    if DBG is not None:
        DBG["oT"] = [rC[c]["oT"] for c in range(NCORES)]
    gnD = np.zeros((128, 64), np.float32)
    gnD[:, 0:16] = _lay(inp["ffn_norm"][1])
    maps = []
    for c in range(NCORES):
        m = {"xT": rB[c]["x2T"], "oT": rC[c]["oT"], "gains": gnD}
        for k in ["wo", "gu1", "dn1"]:
            m["w_" + k] = wb[k]
        maps.append(m)
    rD = _run(build_stageD(), maps, launches, "D")
    out = np.empty((BATCH, SEQ, D), np.float32)
    for c in range(NCORES):
        b, j = c // 4, c % 4
        out[b, j * TPC:(j + 1) * TPC, :] = rD[c]["yT"].T
    return out
```
